# Optimizing a Trainium2 kernel written in Bass

```python
import jax
import jax.numpy as jnp
from jax import lax
import numpy as np

D_MODEL = 2048
BATCH = 4
SEQ = 8192
DEPTH = 2

HEAD_DIM = 128
RET_HEADS = 4
SB_HEADS = 4
SWA_Q_HEADS = 8
SWA_KV_HEADS = 2
D_MIX = (RET_HEADS + SB_HEADS + SWA_Q_HEADS) * HEAD_DIM
RET_CHUNK = 128
SB_BLOCK = 128
WINDOW = 128
CONV_WIDTH = 3
D_FF = 5632
RMS_EPS = 1e-6
GN_EPS = 1e-5
IN_SPLITS = (RET_HEADS * HEAD_DIM,) * 4 + (SB_HEADS * HEAD_DIM,) * 3 + (SWA_Q_HEADS * HEAD_DIM, SWA_KV_HEADS * HEAD_DIM, SWA_KV_HEADS * HEAD_DIM)
D_IN = sum(IN_SPLITS)
SPLIT_IDX = tuple(int(v) for v in np.cumsum(IN_SPLITS)[:-1])

kernel_name = "hymba_style_retention_stickbreak_swa_convffn"


def rms_norm(x, w):
    xf = x.astype(jnp.float32)
    y = xf * lax.rsqrt(jnp.mean(xf * xf, axis=-1, keepdims=True) + RMS_EPS)
    return (y * w.astype(jnp.float32)).astype(x.dtype)


def retention(q, k, v, g, gn_w):
    B, S, H, d = q.shape
    C = RET_CHUNK
    NC = S // C
    log_gamma = jnp.log(1.0 - 2.0 ** (-5.0 - jnp.arange(H, dtype=jnp.float32)))
    qc = q.astype(jnp.float32).reshape(B, NC, C, H, d)
    kc = (k.astype(jnp.float32) * d ** -0.5).reshape(B, NC, C, H, d)
    vc = v.astype(jnp.float32).reshape(B, NC, C, H, d)
    idx = jnp.arange(C, dtype=jnp.float32)
    diff = idx[:, None] - idx[None, :]
    decay_intra = jnp.where(diff >= 0, jnp.exp(log_gamma[:, None, None] * jnp.maximum(diff, 0.0)), 0.0)
    scores = jnp.einsum("bnihd,bnjhd->bnhij", qc, kc) * decay_intra
    o_intra = jnp.einsum("bnhij,bnjhe->bnihe", scores, vc)
    zeta = jnp.exp(log_gamma[:, None] * (C - 1.0 - idx)[None, :])
    chunk_state = jnp.einsum("bnjhd,hj,bnjhe->nbhde", kc, zeta, vc)
    chunk_decay = jnp.exp(log_gamma * C)[None, :, None, None]

    def step(state, contrib):
        return state * chunk_decay + contrib, state

    _, prev_state = lax.scan(step, jnp.zeros((B, H, d, d), jnp.float32), chunk_state)
    xi = jnp.exp(log_gamma[:, None] * (idx + 1.0)[None, :])
    o_cross = jnp.einsum("bnihd,hi,nbhde->bnihe", qc, xi, prev_state)
    o = (o_intra + o_cross).reshape(B, S, H, d)
    mu = jnp.mean(o, axis=-1, keepdims=True)
    var = jnp.mean(jnp.square(o - mu), axis=-1, keepdims=True)
    o = ((o - mu) * lax.rsqrt(var + GN_EPS)).reshape(B, S, H * d) * gn_w.astype(jnp.float32)
    return (jax.nn.silu(g.astype(jnp.float32)) * o).astype(q.dtype)


def stick_breaking(q, k, v):
    B, S, H, d = q.shape
    Q = SB_BLOCK
    NB = S // Q
    scale = d ** -0.5
    q_blocks = jnp.moveaxis(q.reshape(B, NB, Q, H, d), 1, 0)
    vf = v.astype(jnp.float32)
    key_pos = jnp.arange(S)

    def block(args):
        q_blk, blk = args
        z = jnp.einsum("bqhd,bkhd->bhqk", q_blk, k).astype(jnp.float32) * scale
        query_pos = blk * Q + jnp.arange(Q)
        causal = key_pos[None, :] < query_pos[:, None]
        log_beta = jax.nn.log_sigmoid(z)
        log_keep = jnp.where(causal, log_beta - z, 0.0)
        log_survive = lax.cumsum(log_keep, axis=3, reverse=True) - log_keep
        a = jnp.where(causal, jnp.exp(log_beta + log_survive), 0.0)
        return jnp.einsum("bhqk,bkhd->bqhd", a, vf)

    o = lax.map(block, (q_blocks, jnp.arange(NB)))
    return jnp.moveaxis(o, 0, 1).reshape(B, S, H * d).astype(q.dtype)


def sliding_window_sinks(q, k, v, sinks):
    B, S, Hq, d = q.shape
    Hkv = k.shape[2]
    G = Hq // Hkv
    W = WINDOW
    NB = S // W
    qb = q.reshape(B, NB, W, Hkv, G, d)

    def band(t):
        tb = t.reshape(B, NB, W, Hkv, d)
        prev = jnp.pad(tb[:, :-1], ((0, 0), (1, 0), (0, 0), (0, 0), (0, 0)))
        return jnp.concatenate([prev, tb], axis=2)

    kb, vb = band(k), band(v)
    s = jnp.einsum("bnqhgd,bnkhd->bnhgqk", qb, kb).astype(jnp.float32) * d ** -0.5
    qi = jnp.arange(W)[:, None]
    kj = jnp.arange(2 * W)[None, :]
    dist = qi + W - kj
    blk = jnp.arange(NB)[:, None, None]
    valid = (dist >= 0) & (dist < W) & (blk * W + kj - W >= 0)
    slopes = (2.0 ** (-(8.0 / Hq) * (jnp.arange(Hq, dtype=jnp.float32) + 1.0))).reshape(Hkv, G)
    s = s - slopes[:, :, None, None] * dist.astype(jnp.float32)
    s = jnp.where(valid[None, :, None, None], s, -jnp.inf)
    sink = sinks.astype(jnp.float32).reshape(Hkv, G)[:, :, None, None]
    m = jnp.maximum(jnp.max(s, axis=-1, keepdims=True), sink)
    p = jnp.exp(s - m)
    denom = jnp.sum(p, axis=-1, keepdims=True) + jnp.exp(sink - m)
    o = jnp.einsum("bnhgqk,bnkhd->bnqhgd", p / denom, vb.astype(jnp.float32))
    return o.reshape(B, S, Hq * d).astype(q.dtype)


def hybrid_mixer(h, w_in, w_out, ret_gn_w, swa_sinks):
    B, S, _ = h.shape
    proj = h @ w_in
    rq, rk, rv, rg, sq, sk, sv, aq, ak, av = jnp.split(proj, SPLIT_IDX, axis=-1)

    def heads(t, n):
        return t.reshape(B, S, n, HEAD_DIM)

    o_ret = retention(heads(rq, RET_HEADS), heads(rk, RET_HEADS), heads(rv, RET_HEADS), rg, ret_gn_w)
    o_sb = stick_breaking(heads(sq, SB_HEADS), heads(sk, SB_HEADS), heads(sv, SB_HEADS))
    o_swa = sliding_window_sinks(heads(aq, SWA_Q_HEADS), heads(ak, SWA_KV_HEADS), heads(av, SWA_KV_HEADS), swa_sinks)
    return jnp.concatenate([o_ret, o_sb, o_swa], axis=-1) @ w_out


def conv_ffn(h, w_up, conv_w, conv_b, w_down):
    S = h.shape[1]
    a, b = jnp.split(h @ w_up, 2, axis=-1)
    a_pad = jnp.pad(a, ((0, 0), (CONV_WIDTH - 1, 0), (0, 0)))
    a_conv = conv_b
    for i in range(CONV_WIDTH):
        a_conv = a_conv + a_pad[:, i:i + S] * conv_w[i]
    return (jax.nn.gelu(a_conv, approximate=True) * b) @ w_down


def setup_inputs(seed: int = 0) -> dict:
    key = jax.random.key(seed)
    ks = jax.random.split(key, 13)
    L = DEPTH

    def normal(k, shape, scale):
        return jax.random.normal(k, shape, jnp.float32) * scale

    def gain(k, shape):
        return 1.0 + 0.02 * jax.random.normal(k, shape, jnp.float32)

    return {
        "x": normal(ks[0], (BATCH, SEQ, D_MODEL), 1.0),
        "w_in": normal(ks[1], (L, D_MODEL, D_IN), D_MODEL ** -0.5),
        "w_out": normal(ks[2], (L, D_MIX, D_MODEL), D_MIX ** -0.5),
        "ret_gn_w": gain(ks[3], (L, RET_HEADS * HEAD_DIM)),
        "swa_sinks": normal(ks[4], (L, SWA_Q_HEADS), 0.5),
        "norm_mix_pre": gain(ks[5], (L, D_MODEL)),
        "norm_mix_post": gain(ks[6], (L, D_MODEL)),
        "norm_ffn_pre": gain(ks[7], (L, D_MODEL)),
        "norm_ffn_post": gain(ks[8], (L, D_MODEL)),
        "w_up": normal(ks[9], (L, D_MODEL, 2 * D_FF), D_MODEL ** -0.5),
        "conv_w": normal(ks[10], (L, CONV_WIDTH, D_FF), CONV_WIDTH ** -0.5),
        "conv_b": normal(ks[11], (L, D_FF), 0.02),
        "w_down": normal(ks[12], (L, D_FF, D_MODEL), D_FF ** -0.5),
    }


def reference(x, w_in, w_out, ret_gn_w, swa_sinks, norm_mix_pre, norm_mix_post, norm_ffn_pre, norm_ffn_post, w_up, conv_w, conv_b, w_down):
    for l in range(DEPTH):
        h = rms_norm(x, norm_mix_pre[l])
        x = x + rms_norm(hybrid_mixer(h, w_in[l], w_out[l], ret_gn_w[l], swa_sinks[l]), norm_mix_post[l])
        h = rms_norm(x, norm_ffn_pre[l])
        x = x + rms_norm(conv_ffn(h, w_up[l], conv_w[l], conv_b[l], w_down[l]), norm_ffn_post[l])
    return x
```

```python
import contextlib
import numpy as np
import ml_dtypes
import concourse.bass as bass
import concourse.mybir as mybir
from concourse.bass_utils import run_bass_kernel_spmd

F32 = mybir.dt.float32
BF16 = mybir.dt.bfloat16
AF = mybir.ActivationFunctionType
ALU = mybir.AluOpType
AX = mybir.AxisListType
NPBF = ml_dtypes.bfloat16

D = 2048
S = 8192
NB = 4
TOK = 4096
DFF = 5632
DIN = 5120
NFC = DFF // 128
RMS_EPS = 1e-6
GN_EPS = 1e-5
NCORES = 8


class Sem:
    def __init__(self, h, idx):
        self.h = h
        self.idx = idx
        self.val = 0


class Buf:
    __slots__ = ("w", "r")

    def __init__(self):
        self.w = None
        self.r = {}


class Eng:
    def __init__(self, kb, e, name, is_pe=False):
        self.e = e
        self.name = name
        self.is_pe = is_pe
        self.sem = kb.new_sem("s_" + name)
        self.seen = {}
        self.dma_sems = None
        self.dma_i = 0


class KB:
    NDS = 8

    def __init__(self):
        self.nc = bass.Bass("TRN2", target_bir_lowering=False)
        self.st = contextlib.ExitStack()
        self.stack = []
        self.nsem = 0
        nc = self.nc
        self.pe = Eng(self, nc.tensor, "pe", True)
        self.act = Eng(self, nc.scalar, "act")
        self.dve = Eng(self, nc.vector, "dve")
        self.pool = Eng(self, nc.gpsimd, "pool")
        self.sp = Eng(self, nc.sync, "sp")
        for q in (self.sp, self.pool, self.act):
            q.dma_sems = [self.new_sem("d_%s%d" % (q.name, i)) for i in range(self.NDS)]
        self.out_tokens = []

    def new_sem(self, name):
        h = self.st.enter_context(self.nc.semaphore(name))
        self.nsem += 1
        return Sem(h, self.nsem)

    def sb(self, name, shape, dt):
        return self.st.enter_context(self.nc.sbuf_tensor(name, list(shape), dt))

    def ps(self, name, shape, dt=F32):
        return self.st.enter_context(self.nc.psum_tensor(name, list(shape), dt))

    def dram(self, name, shape, dt, kind):
        return self.nc.dram_tensor(name, list(shape), dt, kind=kind).ap()

    def _wait(self, E, tok):
        s, v = tok
        if E.is_pe and s is E.sem:
            return
        if E.seen.get(s.idx, 0) >= v:
            return
        E.e.wait_ge(s.h, v)
        E.seen[s.idx] = v

    def op(self, E, fn, reads=(), writes=(), dma=False):
        for b in reads:
            if b.w is not None:
                self._wait(E, b.w)
        for b in writes:
            if b.w is not None:
                self._wait(E, b.w)
            for t in b.r.values():
                self._wait(E, t)
        if dma:
            s = E.dma_sems[E.dma_i % self.NDS]
            E.dma_i += 1
            if s.val > 0:
                self._wait(E, (s, s.val))
            ins = fn(E.e)
            s.val += 16
            ins.then_inc(s.h, 16)
            tok = (s, s.val)
        else:
            ins = fn(E.e)
            s = E.sem
            s.val += 1
            ins.then_inc(s.h, 1)
            tok = (s, s.val)
        for b in reads:
            b.r[s.idx] = tok
        for b in writes:
            b.w = tok
            b.r = {}
        return tok

    def dma(self, E, out, in_, reads=(), writes=(), **kw):
        return self.op(E, lambda e: e.dma_start(out=out, in_=in_, **kw), reads, writes, dma=True)

    def engines(self):
        return (self.pe, self.act, self.dve, self.pool, self.sp)

    def barrier(self):
        toks = [(E.sem, E.sem.val) for E in self.engines() if E.sem.val > 0]
        for q in (self.sp, self.pool, self.act):
            toks += [(s_, s_.val) for s_ in q.dma_sems if s_.val > 0]
        for E in self.engines():
            for t in toks:
                if t[0] is E.sem:
                    continue
                self._wait(E, t)

    def push(self):
        self.stack.append(self.st)
        self.st = contextlib.ExitStack()

    def pop(self):
        self.barrier()
        self.st.close()
        self.st = self.stack.pop()

    def finish(self):
        for q in (self.sp, self.pool, self.act):
            for s in q.dma_sems:
                if s.val > 0:
                    self._wait(self.sp, (s, s.val))
        self.st.close()
        return self.nc


def bufs(n):
    return [Buf() for _ in range(n)]


A_FBLK = {0: 0, 1: 4, 4: 8, 5: 12, 7: 16, 8: 20}
A_MBLK = {1: 0, 2: 512, 3: 1024, 6: 1536}


def emit_norm_transpose(k, x_dram, nw_col, hT, ident, ntg, xin, xsb, junk, stat, tp, B):
    nc = k.nc
    for tg in range(ntg):
        i = tg % 2
        k.dma(k.sp, xin[i][:], x_dram[tg * 128:(tg + 1) * 128, :], writes=[B["xin"][i]])
        k.op(k.act, lambda e: e.activation(out=xsb[i][:], in_=xin[i][:], func=AF.Square,
                                           accum_out=stat[i][:, 0:1]),
             reads=[B["xin"][i]], writes=[B["xsb"][i], B["stat"][i]])
        k.op(k.dve, lambda e: e.tensor_scalar(out=stat[i][:, 1:2], in0=stat[i][:, 0:1], scalar1=1.0 / D,
                                              scalar2=RMS_EPS, op0=ALU.mult, op1=ALU.add),
             reads=[B["stat"][i]], writes=[B["stat"][i]])
        k.op(k.act, lambda e: e.activation(out=stat[i][:, 2:3], in_=stat[i][:, 1:2], func=AF.Sqrt),
             reads=[B["stat"][i]], writes=[B["stat"][i]])
        k.op(k.dve, lambda e: e.reciprocal(out=stat[i][:, 3:4], in_=stat[i][:, 2:3]),
             reads=[B["stat"][i]], writes=[B["stat"][i]])
        k.op(k.act, lambda e: e.activation(out=xsb[i][:], in_=xin[i][:], func=AF.Copy,
                                           scale=stat[i][:, 3:4]),
             reads=[B["xin"][i], B["stat"][i]], writes=[B["xsb"][i]])
        for kc in range(16):
            k.op(k.pe, lambda e: e.transpose(out=tp[i][:, kc, :], in_=xsb[i][:, kc * 128:(kc + 1) * 128],
                                             identity=ident[:]),
                 reads=[B["xsb"][i], B["ident"]], writes=[B["tp"][i]])
        k.op(k.dve, lambda e: e.tensor_tensor(out=hT[:, :, tg * 128:(tg + 1) * 128], in0=tp[i][:],
                                              in1=nw_col[:, :].unsqueeze(2).broadcast_to([128, 16, 128]),
                                              op=ALU.mult),
             reads=[B["tp"][i], B["nw"]], writes=[B["hT"]])


def load_identity(k, ident_dram, ident, b):
    k.dma(k.sp, ident[:], ident_dram, writes=[b])


def build_phase_a():
    k = KB()
    x = k.dram("x", [TOK, D], F32, "ExternalInput")
    w = k.dram("w_in", [D, DIN], F32, "ExternalInput")
    nw = k.dram("nw", [128, 16], F32, "ExternalInput")
    idn = k.dram("ident", [128, 128], BF16, "ExternalInput")
    pT = k.dram("pT", [26 * 128, TOK], BF16, "ExternalOutput")
    pM = k.dram("pM", [TOK, 2304], BF16, "ExternalOutput")

    hT = k.sb("hT", [128, 16, TOK], BF16)
    ident = k.sb("ident_sb", [128, 128], BF16)
    nw_col = k.sb("nw_col", [128, 16], F32)
    xin = [k.sb("xin%d" % i, [128, D], F32) for i in range(2)]
    xsb = [k.sb("xsb%d" % i, [128, D], BF16) for i in range(2)]
    junk = None
    stat = [k.sb("stat%d" % i, [128, 4], F32) for i in range(2)]
    tp = [k.ps("tp%d" % i, [128, 16, 128], BF16) for i in range(2)]
    B = dict(xin=bufs(2), xsb=bufs(2), junk=Buf(), stat=bufs(2), tp=bufs(2), nw=Buf(), hT=Buf(), ident=Buf())
    load_identity(k, idn, ident, B["ident"])
    k.dma(k.sp, nw_col[:], nw, writes=[B["nw"]])
    emit_norm_transpose(k, x, nw_col, hT, ident, TOK // 128, xin, xsb, junk, stat, tp, B)
    NW = 2
    wb = [k.sb("wb%d" % i, [128, 16, 512], BF16) for i in range(NW)]
    Bw = bufs(NW)
    acc = [k.ps("acc%d" % i, [128, 512], F32) for i in range(4)]
    Bacc = bufs(4)
    stg = [k.sb("stg%d" % i, [128, 512], BF16) for i in range(4)]
    Bstg = bufs(4)
    wv = w.rearrange("(kc p) n -> p kc n", p=128)
    jobs = []
    for blk in range(10):
        jobs.append(blk)
    ai = 0
    for j, blk in enumerate(jobs):
        s = j % NW
        k.dma(k.pool, wb[s][:], wv[:, :, blk * 512:(blk + 1) * 512], writes=[Bw[s]])
        fparts = []
        mparts = []
        if blk in A_FBLK:
            fparts.append((0, 4, A_FBLK[blk]))
        if blk in A_MBLK:
            mparts.append((0, 512, A_MBLK[blk]))
        if blk == 9:
            fparts.append((0, 2, 24))
            mparts.append((256, 256, 2048))
        for (c0, nm, fb) in fparts:
            for m in range(nm):
                for tt in range(TOK // 512):
                    a = ai % 4
                    ai += 1
                    for kc in range(16):
                        k.op(k.pe, lambda e: e.matmul(acc[a][:], lhsT=wb[s][:, kc, c0 + m * 128:c0 + (m + 1) * 128],
                                                      rhs=hT[:, kc, tt * 512:(tt + 1) * 512],
                                                      start=(kc == 0), stop=(kc == 15)),
                             reads=[Bw[s], B["hT"]], writes=[Bacc[a]])
                    ev = k.act if (ai % 2 == 0) else k.dve
                    if ev is k.act:
                        k.op(ev, lambda e: e.activation(out=stg[a][:], in_=acc[a][:], func=AF.Copy),
                             reads=[Bacc[a]], writes=[Bstg[a]])
                    else:
                        k.op(ev, lambda e: e.tensor_copy(out=stg[a][:], in_=acc[a][:]),
                             reads=[Bacc[a]], writes=[Bstg[a]])
                    k.dma(k.sp, pT[(fb + m) * 128:(fb + m + 1) * 128, tt * 512:(tt + 1) * 512], stg[a][:],
                          reads=[Bstg[a]])
        for (c0, ncol, dc) in mparts:
            for tg in range(TOK // 128):
                a = ai % 4
                ai += 1
                for kc in range(16):
                    k.op(k.pe, lambda e: e.matmul(acc[a][:, 0:ncol], lhsT=hT[:, kc, tg * 128:(tg + 1) * 128],
                                                  rhs=wb[s][:, kc, c0:c0 + ncol],
                                                  start=(kc == 0), stop=(kc == 15)),
                         reads=[Bw[s], B["hT"]], writes=[Bacc[a]])
                ev = k.act if (ai % 2 == 0) else k.dve
                if ev is k.act:
                    k.op(ev, lambda e: e.activation(out=stg[a][:, 0:ncol], in_=acc[a][:, 0:ncol], func=AF.Copy),
                         reads=[Bacc[a]], writes=[Bstg[a]])
                else:
                    k.op(ev, lambda e: e.tensor_copy(out=stg[a][:, 0:ncol], in_=acc[a][:, 0:ncol]),
                         reads=[Bacc[a]], writes=[Bstg[a]])
                k.dma(k.sp, pM[tg * 128:(tg + 1) * 128, dc:dc + ncol], stg[a][:, 0:ncol],
                      reads=[Bstg[a]])
    return k.finish()


NEG = -30000.0
RET_NG = 4
SCALE = 128.0 ** -0.5


def host_pm_layout(pM):
    def pmaj(a):
        t = a.shape[0] // 128
        return a.reshape(t, 128, a.shape[1]).transpose(1, 0, 2).reshape(128, t * a.shape[1])
    parts = [pmaj(pM[:, 768:1024])]
    for g in range(4):
        for h in range(2):
            for ki in range(3):
                parts.append(pmaj(pM[g * 2048:(g + 1) * 2048, ki * 256 + h * 128:ki * 256 + (h + 1) * 128]))
    parts.append(pmaj(pM[:, 1024:1152]))
    return np.ascontiguousarray(np.concatenate(parts, axis=1))


def host_consts_b(hh, gn_w_l, sinks_l):
    c = {}
    s_ = np.arange(128)[:, None]
    j_ = np.arange(128)[None, :]
    c["sbU"] = (s_ >= j_).astype(NPBF)
    c["sbLo"] = (s_ < j_).astype(NPBF)
    p = np.arange(128)[:, None, None]
    r = np.arange(4)[None, :, None]
    cc = np.arange(512)[None, None, :]
    c["sbmask"] = np.where(128 * r + p < cc, 0.0, NEG).astype(np.float32)
    dt = np.zeros((128, 2, 128), np.float32)
    xi = np.zeros((128, 2, 128), np.float32)
    zeta = np.zeros((128, 2), np.float32)
    dec = np.zeros((128, 2), np.float32)
    gnw = np.zeros((128, 2), np.float32)
    for hl in range(2):
        h = 2 * hh + hl
        lg = np.log(1.0 - 2.0 ** (-5.0 - h))
        jj = np.arange(128)[:, None]
        ii = np.arange(128)[None, :]
        dt[:, hl, :] = np.where(ii >= jj, np.exp(lg * np.maximum(ii - jj, 0)), 0.0) * SCALE
        xi[:, hl, :] = np.exp(lg * (np.arange(128) + 1.0))[None, :]
        zeta[:, hl] = np.exp(lg * (127.0 - np.arange(128))) * SCALE
        dec[:, hl] = np.exp(lg * 128.0)
        gnw[:, hl] = gn_w_l[h * 128:(h + 1) * 128]
    c["retDT"] = dt
    c["retXI"] = xi
    c["retcol"] = np.concatenate([zeta, dec, gnw], axis=1).astype(np.float32)
    q = np.arange(128)[:, None, None]
    hq = np.arange(4)[None, :, None]
    kc = np.arange(256)[None, None, :]
    dist = q + 128 - kc
    slope = 2.0 ** (-(4 * hh + hq + 1.0))
    valid = (dist >= 0) & (dist < 128)
    bias = np.where(valid, -slope * dist, NEG).astype(np.float32)
    bias0 = np.where(valid & (kc >= 128), -slope * dist, NEG).astype(np.float32)
    c["swabias"] = np.stack([bias0, bias], axis=1).astype(np.float32)
    c["swasink"] = np.broadcast_to(sinks_l[4 * hh:4 * hh + 4][None, :], (128, 4)).astype(np.float32).copy()
    c["ident"] = np.eye(128, dtype=NPBF)
    return c


def emit_sb(k, pT, pM, oT):
    k.push()
    qT = k.sb("sb_qT", [128, 2, S], BF16)
    kT = k.sb("sb_kT", [128, 2, S], BF16)
    v = k.sb("sb_v", [128, 64, 256], BF16)
    U = k.sb("sb_U", [128, 128], BF16)
    Lo = k.sb("sb_Lo", [128, 128], BF16)
    mask = k.sb("sb_mask", [128, 4, 512], F32)
    Bin = Buf()
    for h in range(2):
        k.dma(k.sp, qT[:, h, :], pT[(4 + h) * 128:(5 + h) * 128, :], writes=[Bin])
        k.dma(k.sp, kT[:, h, :], pT[(6 + h) * 128:(7 + h) * 128, :], writes=[Bin])
    k.dma(k.sp, v[:], pM[:, 0:16384].rearrange("p (t c) -> p t c", c=256), writes=[Bin])
    k.dma(k.sp, U[:], k.cdram["sbU"], writes=[Bin])
    k.dma(k.sp, Lo[:], k.cdram["sbLo"], writes=[Bin])
    k.dma(k.sp, mask[:], k.cdram["sbmask"], writes=[Bin])
    NZ, NE, NL, NX, NA = 3, 3, 3, 2, 3
    Z = [k.ps("sb_Z%d" % i, [128, 512]) for i in range(NZ)]
    P = [k.ps("sb_P%d" % i, [128, 512]) for i in range(2)]
    O = [k.ps("sb_O%d" % i, [128, 512]) for i in range(2)]
    Zm = [k.sb("sb_Zm%d" % i, [128, 512], F32) for i in range(2)]
    E = [k.sb("sb_E%d" % i, [128, 512], F32) for i in range(NE)]
    L = [k.sb("sb_L%d" % i, [128, 512], BF16) for i in range(NL)]
    X = [k.sb("sb_X%d" % i, [128, 512], F32) for i in range(NX)]
    A = [k.sb("sb_A%d" % i, [128, 512], BF16) for i in range(NA)]
    og = [k.sb("sb_og%d" % i, [128, 512], BF16) for i in range(2)]
    BZ, BP, BO, BZm, BE, BL, BX, BA, Bog = bufs(NZ), bufs(2), bufs(2), bufs(2), bufs(NE), bufs(NL), bufs(NX), bufs(NA), bufs(2)
    tiles = []
    ci = 0
    for h in range(2):
        for Q in range(16):
            n = 4 * Q + 4
            for idx, kt in enumerate(range(n - 1, -1, -1)):
                r = kt - 4 * Q
                tiles.append(dict(h=h, Q=Q, kt=kt, first=(idx == 0), last=(idx == n - 1),
                                  diag=(r if r >= 0 else None), chain=ci))
            ci += 1
    n = len(tiles)
    ndiag = 0
    for i, t in enumerate(tiles):
        t["i"] = i
        if t["diag"] is not None:
            t["zm"] = ndiag % 2
            ndiag += 1

    def st_Z(t):
        i = t["i"]
        h = t["h"]
        k.op(k.pe, lambda e: e.matmul(Z[i % NZ][:], lhsT=kT[:, h, t["kt"] * 128:(t["kt"] + 1) * 128],
                                      rhs=qT[:, h, t["Q"] * 512:(t["Q"] + 1) * 512], start=True, stop=True),
             reads=[Bin], writes=[BZ[i % NZ]])
        if t["diag"] is not None:
            zi = t["zm"]
            k.op(k.dve, lambda e: e.scalar_tensor_tensor(out=Zm[zi][:], in0=Z[i % NZ][:], scalar=SCALE,
                                                         in1=mask[:, t["diag"], :], op0=ALU.mult, op1=ALU.add),
                 reads=[BZ[i % NZ], Bin], writes=[BZm[zi]])

    def st_EL(t):
        i = t["i"]
        if t["diag"] is not None:
            zi = t["zm"]
            k.op(k.act, lambda e: e.activation(out=E[i % NE][:], in_=Zm[zi][:], func=AF.Exp),
                 reads=[BZm[zi]], writes=[BE[i % NE]])
        else:
            k.op(k.act, lambda e: e.activation(out=E[i % NE][:], in_=Z[i % NZ][:], func=AF.Exp, scale=SCALE),
                 reads=[BZ[i % NZ]], writes=[BE[i % NE]])
        k.op(k.act, lambda e: e.activation(out=L[i % NL][:], in_=E[i % NE][:], func=AF.Ln, bias=k.one_col[:, 0:1]),
             reads=[BE[i % NE], k.Bone], writes=[BL[i % NL]])

    def st_LL(t):
        i = t["i"]
        c = t["chain"] % 2
        if not t["last"]:
            k.op(k.pe, lambda e: e.matmul(P[c][:], lhsT=Lo[:], rhs=L[i % NL][:], start=False, stop=True,
                                          skip_group_check=True),
                 reads=[BL[i % NL], Bin], writes=[BP[c]])

    def st_UL(t):
        i = t["i"]
        c = t["chain"] % 2
        k.op(k.pe, lambda e: e.matmul(P[c][:], lhsT=U[:], rhs=L[i % NL][:], start=t["first"], stop=True,
                                      skip_group_check=True),
             reads=[BL[i % NL], Bin], writes=[BP[c]])

    def st_XA(t):
        i = t["i"]
        c = t["chain"] % 2
        k.op(k.act, lambda e: e.activation(out=X[i % NX][:], in_=P[c][:], func=AF.Exp, scale=-1.0),
             reads=[BP[c]], writes=[BX[i % NX]])
        k.op(k.dve, lambda e: e.tensor_tensor(out=A[i % NA][:], in0=E[i % NE][:], in1=X[i % NX][:], op=ALU.mult),
             reads=[BE[i % NE], BX[i % NX]], writes=[BA[i % NA]])

    def st_AV(t):
        i = t["i"]
        c = t["chain"] % 2
        h = t["h"]
        k.op(k.pe, lambda e: e.matmul(O[c][:], lhsT=v[:, t["kt"], h * 128:(h + 1) * 128], rhs=A[i % NA][:],
                                      start=t["first"], stop=t["last"], skip_group_check=True),
             reads=[BA[i % NA], Bin], writes=[BO[c]])
        if t["last"]:
            k.op(k.dve, lambda e: e.tensor_copy(out=og[c][:], in_=O[c][:]), reads=[BO[c]], writes=[Bog[c]])
            k.dma(k.sp, oT[(2 + h) * 128:(3 + h) * 128, t["Q"] * 512:(t["Q"] + 1) * 512], og[c][:],
                  reads=[Bog[c]])

    for s in range(-2, n + 2):
        if 0 <= s - 1 < n:
            st_LL(tiles[s - 1])
        if 0 <= s < n:
            st_UL(tiles[s])
        if 0 <= s - 2 < n:
            st_AV(tiles[s - 2])
        if 0 <= s + 2 < n:
            st_Z(tiles[s + 2])
        if 0 <= s + 1 < n:
            st_EL(tiles[s + 1])
        if 0 <= s < n:
            st_XA(tiles[s])
    k.pop()


def build_phase_b(parts=("sb", "ret", "swa")):
    k = KB()
    pT = k.dram("pT", [13 * 128, S], BF16, "ExternalInput")
    pM = k.dram("pM", [128, 73728], BF16, "ExternalInput")
    oT = k.dram("oT", [8 * 128, S], BF16, "ExternalOutput")
    shapes = dict(sbU=([128, 128], BF16), sbLo=([128, 128], BF16), sbmask=([128, 4, 512], F32),
                  retDT=([128, 2, 128], F32), retXI=([128, 2, 128], F32), retcol=([128, 6], F32),
                  swabias=([128, 2, 4, 256], F32), swasink=([128, 4], F32), ident=([128, 128], BF16))
    k.cdram = {n_: k.dram(n_, sh, dt, "ExternalInput") for n_, (sh, dt) in shapes.items()}
    k.one_col = k.sb("one_col", [128, 1], F32)
    k.Bone = Buf()
    k.op(k.dve, lambda e: e.memset(k.one_col[:], 1.0), writes=[k.Bone])
    if "sb" in parts:
        emit_sb(k, pT, pM, oT)
    if "ret" in parts:
        emit_ret(k, pT, pM, oT)
    if "swa" in parts:
        emit_swa(k, pT, pM, oT)
    return k.finish()


def emit_ret(k, pT, pM, oT):
    k.push()
    NG, CPG = RET_NG, 16
    DT = k.sb("rt_DT", [128, 2, 128], F32)
    XI = k.sb("rt_XI", [128, 2, 128], F32)
    col = k.sb("rt_col", [128, 6], F32)
    ident = k.sb("rt_id", [128, 128], BF16)
    epsc = k.sb("rt_eps", [128, 1], F32)
    Bc = Buf()
    k.dma(k.sp, DT[:], k.cdram["retDT"], writes=[Bc])
    k.dma(k.sp, XI[:], k.cdram["retXI"], writes=[Bc])
    k.dma(k.sp, col[:], k.cdram["retcol"], writes=[Bc])
    k.dma(k.sp, ident[:], k.cdram["ident"], writes=[Bc])
    k.op(k.dve, lambda e: e.memset(epsc[:], GN_EPS), writes=[Bc])
    slots = []
    for i in range(4):
        slots.append(dict(
            qT=k.sb("rt_qT%d" % i, [128, 2048], BF16), kT=k.sb("rt_kT%d" % i, [128, 2048], BF16),
            kM=k.sb("rt_kM%d" % i, [128, CPG, 128], BF16), vM=k.sb("rt_vM%d" % i, [128, CPG, 128], BF16),
            gM=k.sb("rt_gM%d" % i, [128, CPG, 128], BF16), oS=k.sb("rt_oS%d" % i, [128, 2048], BF16),
            Bq=Buf(), Bk=Buf(), BkM=Buf(), Bv=Buf(), Bg=Buf(), Bo=Buf()))
    st32 = [k.sb("rt_st%d" % h, [128, 128], F32) for h in range(2)]
    sbf = [[k.sb("rt_sb%d_%d" % (h, i), [128, 128], BF16) for i in range(2)] for h in range(2)]
    Bst = bufs(2)
    Bsbf = [bufs(2) for _ in range(2)]
    for h in range(2):
        k.op(k.dve, lambda e: e.memset(st32[h][:], 0.0), writes=[Bst[h]])
        k.op(k.dve, lambda e: e.memset(sbf[h][0][:], 0.0), writes=[Bsbf[h][0]])
    ST = [k.ps("rt_ST%d" % i, [128, 512]) for i in range(2)]
    OP = [k.ps("rt_O%d" % i, [128, 512]) for i in range(3)]
    KV = [k.ps("rt_KV%d" % i, [128, 512]) for i in range(1)]
    TP = [k.ps("rt_TP%d" % i, [128, 1024], BF16) for i in range(2)]
    BST, BOP, BKV, BTP = bufs(2), bufs(3), bufs(1), bufs(2)
    NR = 3
    PT = [k.sb("rt_PT%d" % i, [128, 128], BF16) for i in range(NR)]
    QX = [k.sb("rt_QX%d" % i, [128, 128], BF16) for i in range(NR)]
    T1 = [k.sb("rt_T1%d" % i, [128, 128], F32) for i in range(NR)]
    T2 = [k.sb("rt_T2%d" % i, [128, 128], BF16) for i in range(NR)]
    SM = [k.sb("rt_SM%d" % i, [128, 12], F32) for i in range(4)]
    BPT, BQX, BT1, BT2, BSM = bufs(NR), bufs(NR), bufs(NR), bufs(NR), bufs(4)

    def load_unit(h, g):
        sl = slots[(g % 2) * 2 + h]
        t0 = g * 2048
        k.dma(k.sp, sl["qT"][:], pT[(0 + h) * 128:(1 + h) * 128, t0:t0 + 2048], writes=[sl["Bq"]])
        k.dma(k.sp, sl["kT"][:], pT[(2 + h) * 128:(3 + h) * 128, t0:t0 + 2048], writes=[sl["Bk"]])
        for ki, (nm, bb) in enumerate((("kM", "BkM"), ("vM", "Bv"), ("gM", "Bg"))):
            c0 = 16384 + ((g * 2 + h) * 3 + ki) * 2048
            k.dma(k.sp, sl[nm][:], pM[:, c0:c0 + 2048].rearrange("p (t c) -> p t c", c=128), writes=[sl[bb]])
        k.op(k.act, lambda e: e.activation(out=sl["gM"][:], in_=sl["gM"][:], func=AF.Silu),
             reads=[sl["Bg"]], writes=[sl["Bg"]])
        k.op(k.dve, lambda e: e.tensor_scalar(out=sl["kM"][:], in0=sl["kM"][:], scalar1=col[:, h:h + 1], scalar2=None,
                                              op0=ALU.mult),
             reads=[sl["BkM"], Bc], writes=[sl["BkM"]])

    jobs = []
    for g in range(NG):
        for c in range(CPG):
            for h in range(2):
                jobs.append(dict(h=h, g=g, c=c, n=g * CPG + c))
    for j, jb in enumerate(jobs):
        jb["j"] = j
        jb["sl"] = slots[(jb["g"] % 2) * 2 + jb["h"]]
    nj = len(jobs)

    def stA(jb):
        j, sl, c = jb["j"], jb["sl"], jb["c"]
        cs = slice(c * 128, (c + 1) * 128)
        k.op(k.pe, lambda e: e.matmul(ST[j % 2][:, 0:128], lhsT=sl["kT"][:, cs], rhs=sl["qT"][:, cs], start=True, stop=True),
             reads=[sl["Bk"], sl["Bq"]], writes=[BST[j % 2]])
        k.op(k.pool, lambda e: e.tensor_tensor(out=QX[j % NR][:], in0=sl["qT"][:, cs], in1=XI[:, jb["h"], :], op=ALU.mult),
             reads=[sl["Bq"], Bc], writes=[BQX[j % NR]])

    def stB(jb):
        j, sl, c, h, n = jb["j"], jb["sl"], jb["c"], jb["h"], jb["n"]
        k.op(k.dve, lambda e: e.tensor_tensor(out=PT[j % NR][:], in0=ST[j % 2][:, 0:128], in1=DT[:, h, :], op=ALU.mult),
             reads=[BST[j % 2], Bc], writes=[BPT[j % NR]])
        k.op(k.pe, lambda e: e.matmul(OP[j % 3][:, 0:128], lhsT=PT[j % NR][:], rhs=sl["vM"][:, c, :], start=True, stop=False),
             reads=[BPT[j % NR], sl["Bv"]], writes=[BOP[j % 3]])
        k.op(k.pe, lambda e: e.matmul(OP[j % 3][:, 0:128], lhsT=QX[j % NR][:], rhs=sbf[h][n % 2][:], start=False, stop=True),
             reads=[BQX[j % NR], Bsbf[h][n % 2]], writes=[BOP[j % 3]])
        k.op(k.pe, lambda e: e.matmul(KV[0][:, 0:128], lhsT=sl["kM"][:, c, :], rhs=sl["vM"][:, c, :], start=True, stop=True),
             reads=[sl["BkM"], sl["Bv"]], writes=[BKV[0]])

    def stC(jb):
        j, h, n = jb["j"], jb["h"], jb["n"]
        sm = SM[j % 4]
        k.op(k.dve, lambda e: e.scalar_tensor_tensor(out=st32[h][:], in0=st32[h][:], scalar=col[:, 2 + h:3 + h],
                                                     in1=KV[0][:, 0:128], op0=ALU.mult, op1=ALU.add),
             reads=[BKV[0], Bc, Bst[h]], writes=[Bst[h]])
        k.op(k.act, lambda e: e.activation(out=sbf[h][(n + 1) % 2][:], in_=st32[h][:], func=AF.Copy),
             reads=[Bst[h]], writes=[Bsbf[h][(n + 1) % 2]])
        k.op(k.dve, lambda e: e.bn_stats(out=sm[:, 0:6], in_=OP[j % 3][:, 0:128]), reads=[BOP[j % 3]], writes=[BSM[j % 4]])
        k.op(k.dve, lambda e: e.bn_aggr(out=sm[:, 6:8], in_=sm[:, 0:6]), reads=[BSM[j % 4]], writes=[BSM[j % 4]])
        k.op(k.act, lambda e: e.activation(out=sm[:, 8:9], in_=sm[:, 7:8], func=AF.Sqrt, bias=epsc[:, 0:1]),
             reads=[BSM[j % 4], Bc], writes=[BSM[j % 4]])

    def stD(jb):
        j, sl, c = jb["j"], jb["sl"], jb["c"]
        sm = SM[j % 4]
        k.op(k.dve, lambda e: e.reciprocal(out=sm[:, 9:10], in_=sm[:, 8:9]), reads=[BSM[j % 4]], writes=[BSM[j % 4]])
        k.op(k.dve, lambda e: e.tensor_scalar(out=T1[j % NR][:], in0=OP[j % 3][:, 0:128], scalar1=sm[:, 6:7],
                                              scalar2=sm[:, 9:10], op0=ALU.subtract, op1=ALU.mult),
             reads=[BOP[j % 3], BSM[j % 4]], writes=[BT1[j % NR]])
        k.op(k.pool, lambda e: e.tensor_tensor(out=T2[j % NR][:], in0=T1[j % NR][:], in1=sl["gM"][:, c, :], op=ALU.mult),
             reads=[BT1[j % NR], sl["Bg"]], writes=[BT2[j % NR]])
        k.op(k.pe, lambda e: e.transpose(out=TP[j % 2][:, 0:128], in_=T2[j % NR][:], identity=ident[:]),
             reads=[BT2[j % NR], Bc], writes=[BTP[j % 2]])

    def stE(jb):
        j, sl, c, h, g = jb["j"], jb["sl"], jb["c"], jb["h"], jb["g"]
        k.op(k.act, lambda e: e.activation(out=sl["oS"][:, c * 128:(c + 1) * 128], in_=TP[j % 2][:, 0:128], func=AF.Copy,
                                           scale=col[:, 4 + h:5 + h]),
             reads=[BTP[j % 2], Bc], writes=[sl["Bo"]])
        if c == CPG - 1:
            k.dma(k.sp, oT[h * 128:(h + 1) * 128, g * 2048:(g + 1) * 2048], sl["oS"][:], reads=[sl["Bo"]])

    for g0 in range(min(2, NG)):
        load_unit(0, g0)
        load_unit(1, g0)
    for s in range(nj + 4):
        for off, fn in ((4, stE), (3, stD), (2, stC), (1, stB), (0, stA)):
            if 0 <= s - off < nj:
                fn(jobs[s - off])
        if 0 <= s - 4 < nj:
            jb = jobs[s - 4]
            if jb["c"] == CPG - 1 and jb["h"] == 1 and jb["g"] + 2 < NG:
                load_unit(0, jb["g"] + 2)
                load_unit(1, jb["g"] + 2)
    k.pop()


def emit_swa(k, pT, pM, oT):
    k.push()
    NBK = 64
    qT = k.sb("sw_qT", [128, 4, S], BF16)
    kT = k.sb("sw_kT", [128, S + 128], BF16)
    v = k.sb("sw_v", [128, NBK + 1, 128], BF16)
    bias = k.sb("sw_bias", [128, 2, 4, 256], F32)
    sink = k.sb("sw_sink", [128, 4], F32)
    ident = k.sb("sw_id", [128, 128], BF16)
    Bin = Buf()
    k.op(k.dve, lambda e: e.memset(kT[:, 0:128], 0.0), writes=[Bin])
    k.op(k.dve, lambda e: e.memset(v[:, 0, :], 0.0), writes=[Bin])
    for h in range(4):
        k.dma(k.sp, qT[:, h, :], pT[(8 + h) * 128:(9 + h) * 128, :], writes=[Bin])
    k.dma(k.sp, kT[:, 128:], pT[12 * 128:13 * 128, :], writes=[Bin])
    k.dma(k.sp, v[:, 1:, :], pM[:, 65536:73728].rearrange("p (t c) -> p t c", c=128), writes=[Bin])
    k.dma(k.sp, bias[:], k.cdram["swabias"], writes=[Bin])
    k.dma(k.sp, sink[:], k.cdram["swasink"], writes=[Bin])
    k.dma(k.sp, ident[:], k.cdram["ident"], writes=[Bin])
    SP_ = [k.ps("sw_S%d" % i, [128, 4, 256]) for i in range(2)]
    PTp = [k.ps("sw_PT%d" % i, [128, 8, 128], BF16) for i in range(2)]
    OTp = [k.ps("sw_OT%d" % i, [128, 4, 128]) for i in range(2)]
    BS, BPTp, BOTp = bufs(2), bufs(2), bufs(2)
    s2 = [k.sb("sw_s2%d" % i, [128, 4, 256], F32) for i in range(2)]
    pp = [k.sb("sw_p%d" % i, [128, 4, 256], F32) for i in range(2)]
    pn = [k.sb("sw_pn%d" % i, [128, 4, 256], BF16) for i in range(2)]
    PTs = [k.sb("sw_PTs%d" % i, [128, 8, 128], BF16) for i in range(2)]
    sm = [k.sb("sw_sm%d" % i, [128, 8, 4], F32) for i in range(3)]
    stg = [k.sb("sw_stg%d" % i, [128, 4, 2048], BF16) for i in range(2)]
    Bs2, Bpp, Bpn, BPTs, Bsm, Bstg = bufs(2), bufs(2), bufs(2), bufs(2), bufs(3), bufs(2)

    def stA(n):
        for h in range(4):
            k.op(k.pe, lambda e: e.matmul(SP_[n % 2][:, h, :], lhsT=qT[:, h, n * 128:(n + 1) * 128],
                                          rhs=kT[:, n * 128:n * 128 + 256], start=True, stop=True, skip_group_check=True),
                 reads=[Bin], writes=[BS[n % 2]])

    def stB(n):
        m = sm[n % 3]
        bsel = 0 if n == 0 else 1
        k.op(k.dve, lambda e: e.scalar_tensor_tensor(out=s2[n % 2][:], in0=SP_[n % 2][:], scalar=SCALE,
                                                     in1=bias[:, bsel, :, :], op0=ALU.mult, op1=ALU.add),
             reads=[BS[n % 2], Bin], writes=[Bs2[n % 2]])
        k.op(k.dve, lambda e: e.tensor_reduce(out=m[:, 0, :], in_=s2[n % 2][:], axis=AX.X, op=ALU.max),
             reads=[Bs2[n % 2]], writes=[Bsm[n % 3]])
        k.op(k.dve, lambda e: e.tensor_tensor(out=m[:, 1, :], in0=m[:, 0, :], in1=sink[:], op=ALU.max),
             reads=[Bsm[n % 3], Bin], writes=[Bsm[n % 3]])
        k.op(k.dve, lambda e: e.tensor_scalar(out=m[:, 2, :], in0=m[:, 1, :], scalar1=-1.0, scalar2=None, op0=ALU.mult),
             reads=[Bsm[n % 3]], writes=[Bsm[n % 3]])
        k.op(k.dve, lambda e: e.tensor_tensor(out=m[:, 3, :], in0=m[:, 2, :], in1=sink[:], op=ALU.add),
             reads=[Bsm[n % 3], Bin], writes=[Bsm[n % 3]])

    def stC(n):
        m = sm[n % 3]
        for h in range(4):
            k.op(k.act, lambda e: e.activation(out=pp[n % 2][:, h, :], in_=s2[n % 2][:, h, :], func=AF.Exp,
                                               bias=m[:, 2, h:h + 1], accum_out=m[:, 4, h:h + 1]),
                 reads=[Bs2[n % 2], Bsm[n % 3]], writes=[Bpp[n % 2], Bsm[n % 3]])
        k.op(k.act, lambda e: e.activation(out=m[:, 5, :], in_=m[:, 3, :], func=AF.Exp),
             reads=[Bsm[n % 3]], writes=[Bsm[n % 3]])

    def stD(n):
        m = sm[n % 3]
        k.op(k.dve, lambda e: e.tensor_tensor(out=m[:, 6, :], in0=m[:, 4, :], in1=m[:, 5, :], op=ALU.add),
             reads=[Bsm[n % 3]], writes=[Bsm[n % 3]])
        k.op(k.dve, lambda e: e.reciprocal(out=m[:, 7, :], in_=m[:, 6, :]), reads=[Bsm[n % 3]], writes=[Bsm[n % 3]])
        k.op(k.dve, lambda e: e.tensor_tensor(out=pn[n % 2][:], in0=pp[n % 2][:],
                                              in1=m[:, 7, :].unsqueeze(2).broadcast_to([128, 4, 256]), op=ALU.mult),
             reads=[Bpp[n % 2], Bsm[n % 3]], writes=[Bpn[n % 2]])

    def stE(n):
        for h in range(4):
            for half in range(2):
                k.op(k.pe, lambda e: e.transpose(out=PTp[n % 2][:, 2 * h + half, :],
                                                 in_=pn[n % 2][:, h, half * 128:(half + 1) * 128], identity=ident[:]),
                     reads=[Bpn[n % 2], Bin], writes=[BPTp[n % 2]])
        k.op(k.act, lambda e: e.activation(out=PTs[n % 2][:], in_=PTp[n % 2][:], func=AF.Copy),
             reads=[BPTp[n % 2]], writes=[BPTs[n % 2]])

    def stF(n):
        for h in range(4):
            for half in range(2):
                k.op(k.pe, lambda e: e.matmul(OTp[n % 2][:, h, :], lhsT=v[:, n + half, :], rhs=PTs[n % 2][:, 2 * h + half, :],
                                              start=(half == 0), stop=(half == 1), skip_group_check=True),
                     reads=[BPTs[n % 2], Bin], writes=[BOTp[n % 2]])
        g = n // 16
        c = n % 16
        k.op(k.dve, lambda e: e.tensor_copy(out=stg[g % 2][:, :, c * 128:(c + 1) * 128], in_=OTp[n % 2][:]),
             reads=[BOTp[n % 2]], writes=[Bstg[g % 2]])
        if c == 15:
            for h in range(4):
                k.dma(k.sp, oT[(4 + h) * 128:(5 + h) * 128, g * 2048:(g + 1) * 2048], stg[g % 2][:, h, :],
                      reads=[Bstg[g % 2]])

    for s in range(NBK + 5):
        for off, fn in ((5, stF), (4, stE), (3, stD), (2, stC), (1, stB), (0, stA)):
            if 0 <= s - off < NBK:
                fn(s - off)
    k.pop()


TT = 512
NTT = TOK // TT


def emit_rstd(k, ssq, rstd, Bs, inv_n, eps):
    k.op(k.dve, lambda e: e.tensor_scalar(out=rstd, in0=ssq, scalar1=inv_n, scalar2=eps, op0=ALU.mult, op1=ALU.add),
         reads=[Bs], writes=[Bs])
    k.op(k.act, lambda e: e.activation(out=rstd, in_=rstd, func=AF.Sqrt), reads=[Bs], writes=[Bs])
    k.op(k.dve, lambda e: e.reciprocal(out=rstd, in_=rstd), reads=[Bs], writes=[Bs])


def build_phase_c():
    k = KB()
    oT = k.dram("oT", [D, TOK], BF16, "ExternalInput")
    x = k.dram("x", [TOK, D], F32, "ExternalInput")
    oTh = k.dram("oTh", [D, 128], BF16, "ExternalInput")
    xh = k.dram("xh", [128, D], F32, "ExternalInput")
    w_out = k.dram("w_out", [D, D], F32, "ExternalInput")
    w_up = k.dram("w_up", [D, 2 * DFF], F32, "ExternalInput")
    w_down = k.dram("w_down", [DFF, D], F32, "ExternalInput")
    wp1_d = k.dram("wp1", [128, D], F32, "ExternalInput")
    wp2_d = k.dram("wp2", [128, D], F32, "ExternalInput")
    nw2_d = k.dram("nw2", [128, 16], F32, "ExternalInput")
    cv_d = k.dram("convcol", [128, NFC, 4], F32, "ExternalInput")
    idn_d = k.dram("ident", [128, 128], BF16, "ExternalInput")
    x2 = k.dram("x2", [TOK, D], F32, "ExternalOutput")

    wo_v = w_out.rearrange("(kc p) n -> p kc n", p=128)
    wu_v = w_up.rearrange("(kc p) n -> p kc n", p=128)
    wd_v = w_down.rearrange("(fc p) n -> p fc n", p=128)

    G = k.sb("G", [128, NFC, TT], BF16)
    xs = k.sb("xs", [128, 4, D], F32)
    ys = k.sb("ys", [128, 4, D], F32)
    h2T = k.sb("h2T", [128, 16, TT], BF16)
    NWB = 2
    wb = [k.sb("wb%d" % i, [128, 16, 512], BF16) for i in range(NWB)]
    wp1 = k.sb("wp1s", [128, D], F32)
    wp2 = k.sb("wp2s", [128, D], F32)
    nw2 = k.sb("nw2s", [128, 16], F32)
    cv = k.sb("cvs", [128, NFC, 4], F32)
    ident = k.sb("ids", [128, 128], BF16)
    aH = [k.sb("aH%d" % i, [128, NFC, 2], F32) for i in range(2)]
    h2Th = k.sb("h2Th", [128, 16, 2], BF16)
    hrow = [k.sb("hrow%d" % i, [128, D], BF16) for i in range(2)]
    c1 = [k.sb("c1_%d" % i, [128, TT], F32) for i in range(2)]
    gl = [k.sb("gl_%d" % i, [128, TT], F32) for i in range(2)]
    st = [k.sb("st%d" % i, [128, 8], F32) for i in range(4)]
    PS = [k.ps("PS%d" % i, [128, 512]) for i in range(6)]
    TP = k.ps("TPc", [128, 16, 128], BF16)
    BG, Bxs, Bys, Bh2T, Bcst, BaH0, BaH1, Bh2Th, BTP = Buf(), bufs(4), bufs(4), Buf(), Buf(), Buf(), Buf(), Buf(), Buf()
    BaH = [BaH0, BaH1]
    Bwb, Bhrow, Bc1, Bgl, Bst, BPS = bufs(NWB), bufs(2), bufs(2), bufs(2), bufs(4), bufs(6)
    for dst, src in ((wp1, wp1_d), (wp2, wp2_d), (nw2, nw2_d), (cv, cv_d), (ident, idn_d)):
        k.dma(k.sp, dst[:], src, writes=[Bcst])
    wi = [0]

    def wload(view, i0, n, c0):
        s = wi[0] % NWB
        wi[0] += 1
        k.dma(k.pool, wb[s][:, 0:n, :], view[:, i0:i0 + n, c0:c0 + 512], writes=[Bwb[s]])
        return s

    def tokmajor_proj(view, nch, lhs_fn, ntg, lhs_bufs, xs_, ys_, Bys_):
        for p in range(4):
            ch0 = 0
            while ch0 < nch:
                n = min(16, nch - ch0)
                s = wload(view, ch0, n, p * 512)
                for ci in range(n):
                    ch = ch0 + ci
                    for tg in range(ntg):
                        k.op(k.pe, lambda e: e.matmul(PS[tg][:], lhsT=lhs_fn(ch, tg), rhs=wb[s][:, ci, :],
                                                      start=(ch == 0), stop=(ch == nch - 1)),
                             reads=[Bwb[s]] + lhs_bufs, writes=[BPS[tg]])
                ch0 += n
            for tg in range(ntg):
                if tg % 2 == 0:
                    k.op(k.act, lambda e: e.activation(out=ys_[:, tg, p * 512:(p + 1) * 512], in_=PS[tg][:], func=AF.Copy),
                         reads=[BPS[tg]], writes=[Bys_[tg]])
                else:
                    k.op(k.dve, lambda e: e.tensor_copy(out=ys_[:, tg, p * 512:(p + 1) * 512], in_=PS[tg][:]),
                         reads=[BPS[tg]], writes=[Bys_[tg]])

    def norm_residual(tg, wp, xs_, ys_, Bxs_, Bys_, si):
        s_ = st[si]
        k.op(k.act, lambda e: e.activation(out=hrow[si % 2][:], in_=ys_[:, tg, :], func=AF.Square, accum_out=s_[:, 0:1]),
             reads=[Bys_[tg]], writes=[Bhrow[si % 2], Bst[si]])
        emit_rstd(k, s_[:, 0:1], s_[:, 1:2], Bst[si], 1.0 / D, RMS_EPS)
        k.op(k.dve, lambda e: e.scalar_tensor_tensor(out=ys_[:, tg, :], in0=ys_[:, tg, :], scalar=s_[:, 1:2], in1=wp[:],
                                                     op0=ALU.mult, op1=ALU.mult),
             reads=[Bys_[tg], Bst[si], Bcst], writes=[Bys_[tg]])
        k.op(k.dve, lambda e: e.tensor_tensor(out=xs_[:, tg, :], in0=xs_[:, tg, :], in1=ys_[:, tg, :], op=ALU.add),
             reads=[Bys_[tg], Bxs_[tg]], writes=[Bxs_[tg]])

    def norm_transpose(tg, xs_, Bxs_, dstT, c0, ncol, src0, Bdst, si):
        s_ = st[si]
        hr = hrow[si % 2]
        k.op(k.act, lambda e: e.activation(out=hr[:], in_=xs_[:, tg, :], func=AF.Square, accum_out=s_[:, 2:3]),
             reads=[Bxs_[tg]], writes=[Bhrow[si % 2], Bst[si]])
        emit_rstd(k, s_[:, 2:3], s_[:, 3:4], Bst[si], 1.0 / D, RMS_EPS)
        k.op(k.act, lambda e: e.activation(out=hr[:], in_=xs_[:, tg, :], func=AF.Copy, scale=s_[:, 3:4]),
             reads=[Bxs_[tg], Bst[si]], writes=[Bhrow[si % 2]])
        for kc in range(16):
            k.op(k.pe, lambda e: e.transpose(out=TP[:, kc, :], in_=hr[:, kc * 128:(kc + 1) * 128], identity=ident[:]),
                 reads=[Bhrow[si % 2], Bcst], writes=[BTP])
        k.op(k.dve, lambda e: e.tensor_tensor(out=dstT[:, :, c0:c0 + ncol], in0=TP[:, :, src0:src0 + ncol],
                                              in1=nw2[:, :].unsqueeze(2).broadcast_to([128, 16, ncol]), op=ALU.mult),
             reads=[BTP, Bcst], writes=[Bdst])

    k.dma(k.sp, G[:, 0:16, 0:128], oTh.rearrange("(kc p) t -> p kc t", p=128), writes=[BG])
    k.dma(k.sp, xs[:, 0, :], xh, writes=[Bxs[0]])
    tokmajor_proj(wo_v, 16, lambda ch, tg: G[:, ch, 0:128], 1, [BG], xs, ys, Bys)
    norm_residual(0, wp1, xs, ys, Bxs, Bys, 0)
    norm_transpose(0, xs, Bxs, h2Th, 0, 2, 126, Bh2Th, 0)

    for tt in range(NTT):
        t0 = tt * TT
        k.dma(k.sp, G[:, 0:16, :], oT[:, t0:t0 + TT].rearrange("(kc p) t -> p kc t", p=128), writes=[BG])
        for tg in range(4):
            k.dma(k.sp, xs[:, tg, :], x[t0 + tg * 128:t0 + (tg + 1) * 128, :], writes=[Bxs[tg]])
        tokmajor_proj(wo_v, 16, lambda ch, tg: G[:, ch, tg * 128:(tg + 1) * 128], 4, [BG], xs, ys, Bys)
        for tg in range(4):
            norm_residual(tg, wp1, xs, ys, Bxs, Bys, tg)
        for tg in range(4):
            norm_transpose(tg, xs, Bxs, h2T, tg * 128, 128, 0, Bh2T, tg)
        aHp, aHn = aH[tt % 2], aH[(tt + 1) % 2]
        BaHp, BaHn = BaH[tt % 2], BaH[(tt + 1) % 2]
        fi = 0
        for blk in range(NFC // 4):
            sa = wload(wu_v, 0, 16, blk * 512)
            sb_ = wload(wu_v, 0, 16, DFF + blk * 512)
            for j in range(4):
                fc = blk * 4 + j
                pa, pb = PS[(fi % 2) * 2], PS[(fi % 2) * 2 + 1]
                Bpa, Bpb = BPS[(fi % 2) * 2], BPS[(fi % 2) * 2 + 1]
                ci = fi % 2
                fi += 1
                for kc in range(16):
                    k.op(k.pe, lambda e: e.matmul(pa[:], lhsT=wb[sa][:, kc, j * 128:(j + 1) * 128], rhs=h2T[:, kc, :],
                                                  start=(kc == 0), stop=(kc == 15)),
                         reads=[Bwb[sa], Bh2T], writes=[Bpa])
                for kc in range(16):
                    k.op(k.pe, lambda e: e.matmul(pb[:], lhsT=wb[sb_][:, kc, j * 128:(j + 1) * 128], rhs=h2T[:, kc, :],
                                                  start=(kc == 0), stop=(kc == 15)),
                         reads=[Bwb[sb_], Bh2T], writes=[Bpb])
                if tt == 0:
                    for kc in range(16):
                        k.op(k.pe, lambda e: e.matmul(PS[4][:, 0:2], lhsT=wb[sa][:, kc, j * 128:(j + 1) * 128],
                                                      rhs=h2Th[:, kc, :], start=(kc == 0), stop=(kc == 15)),
                             reads=[Bwb[sa], Bh2Th], writes=[BPS[4]])
                    k.op(k.dve, lambda e: e.tensor_copy(out=aHp[:, fc, :], in_=PS[4][:, 0:2]), reads=[BPS[4]], writes=[BaHp])
                cc = c1[ci]
                k.op(k.act, lambda e: e.activation(out=cc[:], in_=pa[:], func=AF.Identity, scale=cv[:, fc, 2:3],
                                                   bias=cv[:, fc, 3:4]),
                     reads=[Bpa, Bcst], writes=[Bc1[ci]])
                k.op(k.dve, lambda e: e.scalar_tensor_tensor(out=cc[:, 1:TT], in0=pa[:, 0:TT - 1], scalar=cv[:, fc, 1:2],
                                                             in1=cc[:, 1:TT], op0=ALU.mult, op1=ALU.add),
                     reads=[Bpa, Bc1[ci], Bcst], writes=[Bc1[ci]])
                k.op(k.dve, lambda e: e.scalar_tensor_tensor(out=cc[:, 2:TT], in0=pa[:, 0:TT - 2], scalar=cv[:, fc, 0:1],
                                                             in1=cc[:, 2:TT], op0=ALU.mult, op1=ALU.add),
                     reads=[Bpa, Bc1[ci], Bcst], writes=[Bc1[ci]])
                k.op(k.dve, lambda e: e.scalar_tensor_tensor(out=cc[:, 0:2], in0=aHp[:, fc, :], scalar=cv[:, fc, 0:1],
                                                             in1=cc[:, 0:2], op0=ALU.mult, op1=ALU.add),
                     reads=[BaHp, Bc1[ci], Bcst], writes=[Bc1[ci]])
                k.op(k.dve, lambda e: e.scalar_tensor_tensor(out=cc[:, 0:1], in0=aHp[:, fc, 1:2], scalar=cv[:, fc, 1:2],
                                                             in1=cc[:, 0:1], op0=ALU.mult, op1=ALU.add),
                     reads=[BaHp, Bc1[ci], Bcst], writes=[Bc1[ci]])
                k.op(k.dve, lambda e: e.tensor_copy(out=aHn[:, fc, :], in_=pa[:, TT - 2:TT]), reads=[Bpa], writes=[BaHn])
                k.op(k.act, lambda e: e.activation(out=gl[ci][:], in_=cc[:], func=AF.Gelu_apprx_tanh),
                     reads=[Bc1[ci]], writes=[Bgl[ci]])
                k.op(k.dve, lambda e: e.tensor_tensor(out=G[:, fc, :], in0=gl[ci][:], in1=pb[:], op=ALU.mult),
                     reads=[Bgl[ci], Bpb], writes=[BG])
        tokmajor_proj(wd_v, NFC, lambda ch, tg: G[:, ch, tg * 128:(tg + 1) * 128], 4, [BG], xs, ys, Bys)
        for tg in range(4):
            norm_residual(tg, wp2, xs, ys, Bxs, Bys, tg)
            k.dma(k.sp, x2[t0 + tg * 128:t0 + (tg + 1) * 128, :], xs[:, tg, :], reads=[Bxs[tg]])
    return k.finish()


_PROG = {}


def _prog(name, fn):
    if name not in _PROG:
        _PROG[name] = fn()
    return _PROG[name]


def _run(nc, in_maps):
    res = run_bass_kernel_spmd(nc, in_maps, core_ids=list(range(NCORES)))
    return res.results


_B_FBLK = lambda hh: ([0 + 2 * hh, 1 + 2 * hh, 4 + 2 * hh, 5 + 2 * hh, 8 + 2 * hh, 9 + 2 * hh, 12 + 2 * hh, 13 + 2 * hh]
                      + [16 + 4 * hh + i for i in range(4)] + [24 + hh])


def _layer(X, l, p):
    ident = np.eye(128, dtype=NPBF)
    col16 = lambda v: np.ascontiguousarray(v.reshape(16, 128).T.astype(np.float32))
    bc128 = lambda v: np.ascontiguousarray(np.broadcast_to(v[None, :].astype(np.float32), (128, v.shape[0])))
    ncA = _prog("A", build_phase_a)
    w_in = np.ascontiguousarray(p["w_in"][l])
    nw = col16(p["norm_mix_pre"][l])
    ims = []
    for c in range(NCORES):
        b, hf = c // 2, c % 2
        ims.append({"x": np.ascontiguousarray(X[b, hf * TOK:(hf + 1) * TOK]), "w_in": w_in, "nw": nw, "ident": ident})
    ra = _run(ncA, ims)
    ncB = _prog("B", build_phase_b)
    ims = []
    for c in range(NCORES):
        b, hh = c // 2, c % 2
        pT_full = np.concatenate([ra[2 * b]["pT"], ra[2 * b + 1]["pT"]], axis=1).reshape(26, 128, S)
        pM_full = np.concatenate([ra[2 * b]["pM"], ra[2 * b + 1]["pM"]], axis=0)
        pT_b = np.ascontiguousarray(pT_full[_B_FBLK(hh)].reshape(13 * 128, S))
        pM_b = np.concatenate([pM_full[:, 256 * hh:256 * hh + 256], pM_full[:, 512 + 256 * hh:512 + 256 * hh + 256],
                               pM_full[:, 1024 + 256 * hh:1024 + 256 * hh + 256],
                               pM_full[:, 1536 + 256 * hh:1536 + 256 * hh + 256],
                               pM_full[:, 2048 + 128 * hh:2048 + 128 * hh + 128]], axis=1)
        im = {"pT": pT_b, "pM": host_pm_layout(pM_b)}
        im.update(host_consts_b(hh, p["ret_gn_w"][l], p["swa_sinks"][l]))
        ims.append(im)
    rb = _run(ncB, ims)
    ncC = _prog("C", build_phase_c)
    cvc = np.concatenate([p["conv_w"][l].T.reshape(NFC, 128, 3), p["conv_b"][l].reshape(NFC, 128, 1)], axis=2)
    cvc = np.ascontiguousarray(cvc.transpose(1, 0, 2).astype(np.float32))
    common = dict(w_out=np.ascontiguousarray(p["w_out"][l]), w_up=np.ascontiguousarray(p["w_up"][l]),
                  w_down=np.ascontiguousarray(p["w_down"][l]), wp1=bc128(p["norm_mix_post"][l]),
                  wp2=bc128(p["norm_ffn_post"][l]), nw2=col16(p["norm_ffn_pre"][l]), convcol=cvc, ident=ident)
    ims = []
    for c in range(NCORES):
        b, hf = c // 2, c % 2
        o0 = rb[2 * b]["oT"].reshape(8, 128, S)
        o1 = rb[2 * b + 1]["oT"].reshape(8, 128, S)
        oT_full = np.concatenate([o0[0:2], o1[0:2], o0[2:4], o1[2:4], o0[4:8], o1[4:8]], axis=0).reshape(D, S)
        im = dict(common)
        im["oT"] = np.ascontiguousarray(oT_full[:, hf * TOK:(hf + 1) * TOK])
        im["x"] = np.ascontiguousarray(X[b, hf * TOK:(hf + 1) * TOK])
        if hf == 1:
            im["oTh"] = np.ascontiguousarray(oT_full[:, TOK - 128:TOK])
            im["xh"] = np.ascontiguousarray(X[b, TOK - 128:TOK])
        else:
            im["oTh"] = np.zeros((D, 128), NPBF)
            im["xh"] = np.zeros((128, D), np.float32)
        ims.append(im)
    rc = _run(ncC, ims)
    Xn = np.empty_like(X)
    for c in range(NCORES):
        b, hf = c // 2, c % 2
        Xn[b, hf * TOK:(hf + 1) * TOK] = rc[c]["x2"]
    return Xn


def kernel(x, w_in, w_out, ret_gn_w, swa_sinks, norm_mix_pre, norm_mix_post, norm_ffn_pre, norm_ffn_post,
           w_up, conv_w, conv_b, w_down):
    p = dict(w_in=np.asarray(w_in), w_out=np.asarray(w_out), ret_gn_w=np.asarray(ret_gn_w),
             swa_sinks=np.asarray(swa_sinks), norm_mix_pre=np.asarray(norm_mix_pre),
             norm_mix_post=np.asarray(norm_mix_post), norm_ffn_pre=np.asarray(norm_ffn_pre),
             norm_ffn_post=np.asarray(norm_ffn_post), w_up=np.asarray(w_up), conv_w=np.asarray(conv_w),
             conv_b=np.asarray(conv_b), w_down=np.asarray(w_down))
    X = np.ascontiguousarray(np.asarray(x, dtype=np.float32))
    for l in range(2):
        X = _layer(X, l, p)
    return X
```
